# Optimizing a Trainium2 kernel written in Bass

```python
import math
import jax, jax.numpy as jnp
from jax import lax
import numpy as np

D_MODEL = 1024
BATCH = 1
SEQ = 16384
DEPTH = 1
DEC_BATCH = 32
DEC_SEQ = 64
PAST_LEN = 2048

CHUNK = 64
GDN_HEADS = 8
GDN_DK = 128
GDN_DV = 128
CONV_W = 4
CONV_DIM = GDN_HEADS * (2 * GDN_DK + GDN_DV)
SWA_HEADS = 16
SWA_KV_HEADS = 2
SWA_GROUP = SWA_HEADS // SWA_KV_HEADS
SWA_HD = 64
WINDOW = 128
NUM_BUCKETS = 32
MAX_DISTANCE = 128
D_FF = 2816
N_MOD = 9
EPS = 1e-6
IN_SPLITS = (CONV_DIM, GDN_HEADS * GDN_DV, GDN_HEADS, GDN_HEADS,
             SWA_HEADS * SWA_HD, SWA_KV_HEADS * SWA_HD, SWA_KV_HEADS * SWA_HD,
             D_MODEL, D_MODEL)
IN_COLS = sum(IN_SPLITS)

kernel_name = 'hybrid_gdn_swa_macaron_adaln_step'


def rmsnorm(x, g):
    x32 = x.astype(jnp.float32)
    y = x32 * lax.rsqrt(jnp.mean(x32 * x32, axis=-1, keepdims=True) + EPS)
    return (y * g.astype(jnp.float32)).astype(x.dtype)


def l2norm(x):
    return x * lax.rsqrt(jnp.sum(x * x, axis=-1, keepdims=True) + EPS)


def swiglu(h, w_in, w_out):
    gate, up = jnp.split(h @ w_in, 2, axis=-1)
    return (jax.nn.silu(gate) * up) @ w_out


def t5_bucket(rel):
    half = NUM_BUCKETS // 2
    max_exact = half // 2
    n = jnp.abs(rel)
    large = max_exact + (jnp.log(jnp.maximum(n, 1).astype(jnp.float32) / max_exact)
                         / math.log(MAX_DISTANCE / max_exact) * (half - max_exact)).astype(jnp.int32)
    large = jnp.minimum(large, half - 1)
    return jnp.where(rel > 0, half, 0) + jnp.where(n < max_exact, n, large)


def gated_delta_chunked(q, k, v, g, beta, s0, C):
    B, L, H, DK = k.shape
    DV = v.shape[-1]
    N = L // C

    def chunks(t):
        return t.reshape(B, N, C, H, t.shape[-1]).transpose(1, 0, 3, 2, 4)

    qc, kc, vc = chunks(q), chunks(k), chunks(v)
    gc = jnp.cumsum(chunks(g[..., None])[..., 0], axis=-1)
    bc = chunks(beta[..., None])
    causal = jnp.tril(jnp.ones((C, C), dtype=bool))
    strict = jnp.tril(jnp.ones((C, C), dtype=bool), k=-1)
    diff = gc[..., :, None] - gc[..., None, :]
    decay = jnp.where(causal, jnp.exp(jnp.where(causal, diff, 0.0)), 0.0)
    kb = kc * bc
    a_mat = jnp.where(strict, jnp.einsum('nbhid,nbhjd->nbhij', kb, kc) * decay, 0.0) + jnp.eye(C, dtype=jnp.float32)
    rhs = jnp.concatenate([vc * bc, kb * jnp.exp(gc)[..., None]], axis=-1)
    sol = lax.linalg.triangular_solve(a_mat, rhs, left_side=True, lower=True)
    u, w = sol[..., :DV], sol[..., DV:]
    intra = jnp.where(causal, jnp.einsum('nbhid,nbhjd->nbhij', qc, kc) * decay, 0.0)

    def step(S, xs):
        q_i, k_i, u_i, w_i, g_i, a_i = xs
        v_new = u_i - jnp.einsum('bhcd,bhde->bhce', w_i, S)
        o = (jnp.einsum('bhcd,bhde->bhce', q_i * jnp.exp(g_i)[..., None], S)
             + jnp.einsum('bhij,bhje->bhie', a_i, v_new))
        g_last = g_i[..., -1:]
        S = (S * jnp.exp(g_last)[..., None]
             + jnp.einsum('bhcd,bhce->bhde', k_i * jnp.exp(g_last - g_i)[..., None], v_new))
        return S, o

    s_final, o = lax.scan(step, s0, (qc, kc, u, w, gc, intra))
    return o.transpose(1, 0, 3, 2, 4).reshape(B, L, H, DV), s_final


def gated_deltanet(qkv_pre, z, b_raw, a_raw, conv_hist, s0, conv_w, a_log, dt_bias, norm_w):
    B, L, _ = qkv_pre.shape
    xp = jnp.concatenate([conv_hist.astype(qkv_pre.dtype), qkv_pre], axis=1)
    conv = lax.conv_general_dilated(xp, conv_w[:, None, :].astype(xp.dtype), window_strides=(1,),
                                    padding='VALID', dimension_numbers=('NWC', 'WIO', 'NWC'),
                                    feature_group_count=CONV_DIM)
    qkv = jax.nn.silu(conv.astype(jnp.float32))
    q, k, v = jnp.split(qkv, [GDN_HEADS * GDN_DK, 2 * GDN_HEADS * GDN_DK], axis=-1)
    q = l2norm(q.reshape(B, L, GDN_HEADS, GDN_DK)) * GDN_DK ** -0.5
    k = l2norm(k.reshape(B, L, GDN_HEADS, GDN_DK))
    v = v.reshape(B, L, GDN_HEADS, GDN_DV)
    beta = jax.nn.sigmoid(b_raw.astype(jnp.float32))
    g = -jnp.exp(a_log.astype(jnp.float32)) * jax.nn.softplus(a_raw.astype(jnp.float32) + dt_bias.astype(jnp.float32))
    o, s_new = gated_delta_chunked(q, k, v, g, beta, s0.astype(jnp.float32), min(CHUNK, L))
    zg = jax.nn.silu(z.astype(jnp.float32).reshape(B, L, GDN_HEADS, GDN_DV))
    o = o * lax.rsqrt(jnp.mean(o * o, axis=-1, keepdims=True) + EPS) * norm_w.astype(jnp.float32) * zg
    return o.reshape(B, L, GDN_HEADS * GDN_DV).astype(qkv_pre.dtype), xp[:, L:], s_new.astype(s0.dtype)


def sliding_window_attention(q, k, v, k_hist, v_hist, n_hist_valid, rel_bias, sinks):
    B, L = q.shape[:2]
    win = k_hist.shape[1]
    C = min(CHUNK, L)
    N = L // C
    k = k.reshape(B, L, SWA_KV_HEADS, SWA_HD)
    v = v.reshape(B, L, SWA_KV_HEADS, SWA_HD)
    kf = jnp.concatenate([k_hist.astype(k.dtype), k], axis=1)
    vf = jnp.concatenate([v_hist.astype(v.dtype), v], axis=1)
    idx = (jnp.arange(N) * C)[:, None] + jnp.arange(win + C)[None, :]
    kb = kf[:, idx]
    vb = vf[:, idx]
    qb = q.reshape(B, N, C, SWA_KV_HEADS, SWA_GROUP, SWA_HD)
    logits = jnp.einsum('bnikgd,bnjkd->bnkgij', qb, kb).astype(jnp.float32) * SWA_HD ** -0.5
    rel = jnp.arange(win + C)[None, :] - win - jnp.arange(C)[:, None]
    bias = rel_bias.astype(jnp.float32)[t5_bucket(rel)]
    bias = bias.transpose(2, 0, 1).reshape(SWA_KV_HEADS, SWA_GROUP, C, win + C)
    valid = (idx >= win - n_hist_valid)[None, :, None, None, None, :]
    logits = jnp.where(valid, logits + bias, -jnp.inf)
    sink = sinks.astype(jnp.float32).reshape(SWA_KV_HEADS, SWA_GROUP, 1, 1)
    m = jnp.maximum(jnp.max(logits, axis=-1, keepdims=True), sink)
    p = jnp.exp(logits - m)
    probs = p / (jnp.sum(p, axis=-1, keepdims=True) + jnp.exp(sink - m))
    out = jnp.einsum('bnkgij,bnjkd->bnikgd', probs.astype(v.dtype), vb)
    return out.reshape(B, L, SWA_HEADS * SWA_HD), kf[:, -win:], vf[:, -win:]


def trunk(x, c, conv_hist, s_hist, k_hist, v_hist, n_hist_valid, p):
    B, L, D = x.shape
    splits = np.cumsum(IN_SPLITS)[:-1].tolist()
    new_conv, new_s, new_k, new_v = [], [], [], []
    for l in range(DEPTH):
        mod = (jax.nn.silu(c) @ p['w_ada'][l] + p['b_ada'][l]).reshape(B, N_MOD, 1, D)
        sh1, sc1, ga1, sh2, sc2, ga2, sh3, sc3, ga3 = [mod[:, i] for i in range(N_MOD)]
        h = rmsnorm(x, p['norm_ffn1'][l]) * (1 + sc1) + sh1
        x = x + 0.5 * ga1 * swiglu(h, p['w_ffn1_in'][l], p['w_ffn1_out'][l])
        h = rmsnorm(x, p['norm_mix'][l]) * (1 + sc2) + sh2
        qkv_a, z_a, b_a, a_a, q_b, k_b, v_b, gate_a, gate_b = jnp.split(h @ p['w_in'][l], splits, axis=-1)
        o_a, conv_new, s_new = gated_deltanet(qkv_a, z_a, b_a, a_a, conv_hist[l], s_hist[l], p['gdn_conv_w'][l],
                                              p['gdn_a_log'][l], p['gdn_dt_bias'][l], p['gdn_norm_w'][l])
        o_b, k_new, v_new = sliding_window_attention(q_b, k_b, v_b, k_hist[l], v_hist[l], n_hist_valid,
                                                     p['rel_bias'], p['swa_sinks'][l])
        merged = (jax.nn.sigmoid(gate_a) * (o_a @ p['w_branch_a'][l])
                  + jax.nn.sigmoid(gate_b) * (o_b @ p['w_branch_b'][l]))
        x = x + ga2 * (merged @ p['w_out'][l])
        h = rmsnorm(x, p['norm_ffn2'][l]) * (1 + sc3) + sh3
        x = x + 0.5 * ga3 * swiglu(h, p['w_ffn2_in'][l], p['w_ffn2_out'][l])
        new_conv.append(conv_new)
        new_s.append(s_new)
        new_k.append(k_new)
        new_v.append(v_new)
    y = rmsnorm(x, p['norm_final'])
    return y, jnp.stack(new_conv), jnp.stack(new_s), jnp.stack(new_k), jnp.stack(new_v)


def setup_inputs(seed: int = 0) -> dict:
    key = jax.random.key(seed)
    ks = iter(jax.random.split(key, 40))

    def nrm(shape, scale):
        return scale * jax.random.normal(next(ks), shape, jnp.float32)

    def gain(shape):
        return 1.0 + nrm(shape, 0.02)

    win_rows = min(WINDOW, PAST_LEN)
    dt = jnp.exp(jax.random.uniform(next(ks), (DEPTH, GDN_HEADS), jnp.float32, math.log(1e-3), math.log(1e-1)))
    a_init = jax.random.uniform(next(ks), (DEPTH, GDN_HEADS), jnp.float32, 1.0, 16.0)
    return {
        'x_prompt': nrm((BATCH, SEQ, D_MODEL), 1.0),
        'x_sample': nrm((DEC_BATCH, DEC_SEQ, D_MODEL), 1.0),
        'state_gdn_conv': nrm((DEPTH, DEC_BATCH, CONV_W - 1, CONV_DIM), 1.0),
        'state_gdn_s': nrm((DEPTH, DEC_BATCH, GDN_HEADS, GDN_DK, GDN_DV), 0.1),
        'cache_swa_k': nrm((DEPTH, DEC_BATCH, win_rows, SWA_KV_HEADS, SWA_HD), 1.0),
        'cache_swa_v': nrm((DEPTH, DEC_BATCH, win_rows, SWA_KV_HEADS, SWA_HD), 1.0),
        'c_prompt': nrm((BATCH, D_MODEL), 1.0),
        'c_sample': nrm((DEC_BATCH, D_MODEL), 1.0),
        'norm_ffn1': gain((DEPTH, D_MODEL)),
        'w_ffn1_in': nrm((DEPTH, D_MODEL, 2 * D_FF), D_MODEL ** -0.5),
        'w_ffn1_out': nrm((DEPTH, D_FF, D_MODEL), D_FF ** -0.5),
        'norm_mix': gain((DEPTH, D_MODEL)),
        'w_in': nrm((DEPTH, D_MODEL, IN_COLS), D_MODEL ** -0.5),
        'gdn_conv_w': nrm((DEPTH, CONV_W, CONV_DIM), 0.5),
        'gdn_a_log': jnp.log(a_init),
        'gdn_dt_bias': dt + jnp.log(-jnp.expm1(-dt)),
        'gdn_norm_w': gain((DEPTH, GDN_DV)),
        'swa_sinks': nrm((DEPTH, SWA_HEADS), 1.0),
        'rel_bias': nrm((NUM_BUCKETS, SWA_HEADS), 0.5),
        'w_branch_a': nrm((DEPTH, GDN_HEADS * GDN_DV, D_MODEL), (GDN_HEADS * GDN_DV) ** -0.5),
        'w_branch_b': nrm((DEPTH, SWA_HEADS * SWA_HD, D_MODEL), (SWA_HEADS * SWA_HD) ** -0.5),
        'w_out': nrm((DEPTH, D_MODEL, D_MODEL), D_MODEL ** -0.5),
        'norm_ffn2': gain((DEPTH, D_MODEL)),
        'w_ffn2_in': nrm((DEPTH, D_MODEL, 2 * D_FF), D_MODEL ** -0.5),
        'w_ffn2_out': nrm((DEPTH, D_FF, D_MODEL), D_FF ** -0.5),
        'w_ada': nrm((DEPTH, D_MODEL, N_MOD * D_MODEL), 0.5 * D_MODEL ** -0.5),
        'b_ada': nrm((DEPTH, N_MOD * D_MODEL), 0.02),
        'norm_final': gain((D_MODEL,)),
    }


def reference(x_prompt, x_sample, state_gdn_conv, state_gdn_s, cache_swa_k, cache_swa_v, c_prompt, c_sample,
              norm_ffn1, w_ffn1_in, w_ffn1_out, norm_mix, w_in, gdn_conv_w, gdn_a_log, gdn_dt_bias, gdn_norm_w,
              swa_sinks, rel_bias, w_branch_a, w_branch_b, w_out, norm_ffn2, w_ffn2_in, w_ffn2_out,
              w_ada, b_ada, norm_final):
    params = {
        'norm_ffn1': norm_ffn1, 'w_ffn1_in': w_ffn1_in, 'w_ffn1_out': w_ffn1_out,
        'norm_mix': norm_mix, 'w_in': w_in, 'gdn_conv_w': gdn_conv_w, 'gdn_a_log': gdn_a_log,
        'gdn_dt_bias': gdn_dt_bias, 'gdn_norm_w': gdn_norm_w, 'swa_sinks': swa_sinks, 'rel_bias': rel_bias,
        'w_branch_a': w_branch_a, 'w_branch_b': w_branch_b, 'w_out': w_out,
        'norm_ffn2': norm_ffn2, 'w_ffn2_in': w_ffn2_in, 'w_ffn2_out': w_ffn2_out,
        'w_ada': w_ada, 'b_ada': b_ada, 'norm_final': norm_final,
    }
    bp = x_prompt.shape[0]
    dt = x_prompt.dtype
    zero_conv = jnp.zeros((DEPTH, bp, CONV_W - 1, CONV_DIM), dt)
    zero_s = jnp.zeros((DEPTH, bp, GDN_HEADS, GDN_DK, GDN_DV), state_gdn_s.dtype)
    zero_kv = jnp.zeros((DEPTH, bp, WINDOW, SWA_KV_HEADS, SWA_HD), dt)
    y_prompt, p_conv, p_s, p_k, p_v = trunk(x_prompt, c_prompt, zero_conv, zero_s, zero_kv, zero_kv, 0, params)
    y_sample, s_conv, s_s, s_k, s_v = trunk(x_sample, c_sample, state_gdn_conv, state_gdn_s, cache_swa_k,
                                            cache_swa_v, cache_swa_k.shape[2], params)
    return (y_prompt, y_sample, p_conv, p_s, p_k, p_v, s_conv, s_s, s_k, s_v)
```

```python
import numpy as np
import concourse.bass as bass
import concourse.mybir as mybir
from concourse.bass_utils import run_bass_kernel_spmd

F32 = mybir.dt.float32
BF16 = mybir.dt.bfloat16
AF = mybir.ActivationFunctionType
ALU = mybir.AluOpType

NCORES = 8
D = 1024
KC = 8
DFF = 2816
HC = 22
NPR = 2048
HALO = 128
NSQ = 4
LSQ = 64
NSA = NSQ * LSQ
NOWN = NPR + NSA
EPS = 1e-6
NEG = -30000.0
TS = 256
import os as _os
SEM_LIMIT = int(_os.environ.get("SEM_LIMIT", "16000"))

C_QKV, C_Z, C_B, C_A, C_QS, C_KS, C_VS, C_GA, C_GB = 0, 3072, 4096, 4104, 4112, 5136, 5264, 5392, 6416

CST = {}
_o = 0
for _n, _w in [("ident", 128), ("ublk", 128), ("blk", 128), ("maskl", 128), ("masku", 128), ("onesblk", 256),
               ("norm1", 8), ("norm2", 8), ("norm3", 8), ("normf", 8), ("bada", 72), ("convw", 96), ("dtb", 8),
               ("alog", 8), ("gnw", 1), ("sinks", 16), ("hmask", 1), ("flag", 1), ("rankmask", 8), ("cT", 40),
               ("eps", 1), ("one", 1), ("eps128", 1)]:
    CST[_n] = (_o, _w)
    _o += _w
NCST = _o


class Buf:
    __slots__ = ("name", "lw", "rd")

    def __init__(self, name):
        self.name = name
        self.lw = None
        self.rd = {}


class Eng:
    def __init__(self, S, name, eng, same_wait=True):
        self.S = S
        self.name = name
        self.e = eng
        self.same_wait = same_wait
        self.seen = {}
        self.nwaits = 0
        self.total = 0
        self.newsem()

    def newsem(self):
        self.sem = self.S.nc.alloc_semaphore("sem_%s_%d" % (self.name, self.total))
        self.S.sems[id(self.sem)] = self.sem
        self.count = 0


class Sched:
    def __init__(self, nc):
        self.nc = nc
        self.sems = {}
        self.pe = Eng(self, "pe", nc.tensor, same_wait=False)
        self.act = Eng(self, "act", nc.scalar)
        self.dve = Eng(self, "dve", nc.vector)
        self.pool = Eng(self, "pool", nc.gpsimd)
        self.sp = Eng(self, "sp", nc.sync)
        self.engs = [self.pe, self.act, self.dve, self.pool, self.sp]
        self.dma_sems = {}
        self.dma_counts = {}
        self.dma_gen = {}
        self.n_ins = 0

    def _wait(self, E, deps):
        for (sid, val) in deps:
            if sid == id(E.sem) and not E.same_wait:
                continue
            if E.seen.get(sid, 0) < val:
                E.e.wait_ge(self.sems[sid], val)
                E.seen[sid] = val
                E.nwaits += 1

    def _deps(self, reads, writes):
        deps = []
        for b in reads:
            if b.lw is not None:
                deps.append(b.lw)
        for b in writes:
            if b.lw is not None:
                deps.append(b.lw)
            for sid, v in b.rd.items():
                deps.append((sid, v))
        return deps

    def _mark(self, tok, reads, writes):
        sid, v = tok
        for b in reads:
            if b.rd.get(sid, 0) < v:
                b.rd[sid] = v
        for b in writes:
            b.lw = tok
            b.rd = {}

    def op(self, E, fn, reads=(), writes=()):
        if E.count >= SEM_LIMIT:
            E.newsem()
        self._wait(E, self._deps(reads, writes))
        ins = fn(E.e)
        E.count += 1
        E.total += 1
        ins.then_inc(E.sem, 1)
        tok = (id(E.sem), E.count)
        self._mark(tok, reads, writes)
        self.n_ins += 1
        return tok

    def _dsem(self, key):
        if key not in self.dma_sems or self.dma_counts[key] >= SEM_LIMIT:
            g = self.dma_gen.get(key, 0)
            self.dma_gen[key] = g + 1
            s = self.nc.alloc_semaphore("dq_%s_%d" % (key, g))
            self.dma_sems[key] = s
            self.dma_counts[key] = 0
            self.sems[id(s)] = s
        return self.dma_sems[key]

    def dma(self, E, key, out, in_, reads=(), writes=(), **kw):
        s = self._dsem(key)
        deps = [d for d in self._deps(reads, writes) if d[0] != id(s)]
        self._wait(E, deps)
        ins = E.e.dma_start(out=out, in_=in_, **kw)
        self.dma_counts[key] += 16
        ins.then_inc(s, 16)
        tok = (id(s), self.dma_counts[key])
        self._mark(tok, reads, writes)
        self.n_ins += 1
        return tok

    def wait_all_dma(self, E):
        for key, s in self.dma_sems.items():
            v = self.dma_counts[key]
            if v > 0 and E.seen.get(id(s), 0) < v:
                E.e.wait_ge(s, v)
                E.seen[id(s)] = v


class _Stop(Exception):
    pass


class Tl:
    __slots__ = ("h", "b")

    def __init__(self, h, b):
        self.h = h
        self.b = b


class Builder:
    def __init__(self, stop=None, debug=False):
        self.stop = stop
        self.debug = debug
        self.dbg_outs = {}
        self.nc = bass.Bass("TRN2", target_bir_lowering=False)
        self.S = Sched(self.nc)
        self.psi = 0
        self.wsi = 0

    def sb(self, name, shape, dt=F32):
        return Tl(self.nc.alloc_sbuf_tensor("s_" + name, list(shape), dt), Buf(name))

    def dram(self, name, shape, dt, kind):
        return Tl(self.nc.dram_tensor(name, list(shape), dt, kind=kind).ap(), Buf(name))

    def mm(self, out, lhsT, rhs, start, stop, r, w):
        self.S.op(self.S.pe, lambda e: e.matmul(out, lhsT, rhs, start=start, stop=stop), reads=r, writes=w)

    def act(self, out, in_, func, r, w, bias=None, scale=None):
        kw = {}
        if bias is not None:
            kw["bias"] = bias
        if scale is not None:
            kw["scale"] = scale
        self.S.op(self.S.act, lambda e: e.activation(out=out, in_=in_, func=func, **kw), reads=r, writes=w)

    def tt(self, out, in0, in1, op, r, w, eng=None):
        E = eng or self.S.dve
        self.S.op(E, lambda e: e.tensor_tensor(out=out, in0=in0, in1=in1, op=op), reads=r, writes=w)

    def ts(self, out, in0, s1, s2, op0, op1, r, w, eng=None):
        E = eng or self.S.dve
        if s2 is None:
            self.S.op(E, lambda e: e.tensor_scalar(out=out, in0=in0, scalar1=s1, scalar2=None, op0=op0), reads=r, writes=w)
        else:
            self.S.op(E, lambda e: e.tensor_scalar(out=out, in0=in0, scalar1=s1, scalar2=s2, op0=op0, op1=op1), reads=r, writes=w)

    def stt(self, out, in0, scalar, in1, op0, op1, r, w):
        self.S.op(self.S.dve, lambda e: e.scalar_tensor_tensor(out=out, in0=in0, scalar=scalar, in1=in1, op0=op0, op1=op1),
                  reads=r, writes=w)

    def cp(self, out, in_, r, w, eng=None):
        E = eng or self.S.dve
        if E is self.S.act:
            self.S.op(E, lambda e: e.copy(out=out, in_=in_), reads=r, writes=w)
        else:
            self.S.op(E, lambda e: e.tensor_copy(out=out, in_=in_), reads=r, writes=w)

    def recip(self, out, in_, r, w):
        self.S.op(self.S.dve, lambda e: e.reciprocal(out=out, in_=in_), reads=r, writes=w)

    def memset(self, ap, val, w, eng=None):
        E = eng or self.S.dve
        self.S.op(E, lambda e: e.memset(ap, val), reads=(), writes=w)

    def ps(self, nb=1):
        if nb == 1:
            b = self.psi % 8
            self.psi += 1
            return self.pst[b // 2], (b % 2) * 512, [self.psb[b]]
        if self.psi % 2:
            self.psi += 1
        b = self.psi % 8
        self.psi += 2
        return self.pst[b // 2], 0, [self.psb[b], self.psb[b + 1]]

    def wtile(self, name, j):
        scr, nk, tiles = self.W[name]
        ntot = tiles[j]
        i = self.wsi % len(self.wsl)
        self.wsi += 1
        ws = self.wsl[i]
        self.S.dma(self.S.sp, "w%d" % i, ws.h[:, 0:nk * ntot], scr.h[j * 128:(j + 1) * 128, 0:nk * ntot], reads=[scr.b], writes=[ws.b])
        return ws.h[:, 0:nk * ntot].rearrange("p (k n) -> p k n", k=nk), ws

    def dbg(self, name, tl, ap, shape):
        if not self.debug or name in self.dbg_outs:
            return
        d = self.dram("dbg_" + name, shape, F32 if ap.dtype == F32 else BF16, "ExternalOutput")
        self.dbg_outs[name] = d
        self.S.dma(self.S.pool, "dbg", d.h, ap, reads=[tl.b], writes=[d.b])

    def finish(self):
        S = self.S
        S.wait_all_dma(S.sp)
        S._wait(S.sp, [(id(e.sem), e.count) for e in S.engs if e is not S.sp and e.count > 0])
        return self.nc

    def build(self):
        try:
            return self._build()
        except _Stop:
            return self.finish()

    def chk(self, code):
        if self.stop == code:
            raise _Stop()

    def _build(self):
        nc, S = self.nc, self.S
        I = {}
        for nm, shp in [("xT", [D, HALO + NPR + NSA]), ("w1i", [D, 2 * DFF]), ("w1o", [DFF, D]), ("win", [D, 7440]),
                        ("wa", [D, D]), ("wb", [D, D]), ("wo", [D, D]), ("w2i", [D, 2 * DFF]), ("w2o", [DFF, D]),
                        ("wada", [D, 9 * D]), ("cst", [128, NCST]), ("oh", [32, 64 * 192]), ("relb", [32, 16]),
                        ("esk", [128, 16]), ("convst", [128, 24 * NSQ * 3]), ("s0", [128, NSQ * 8 * 128]),
                        ("kcA", [128, NSQ * 128]), ("kcB", [128, NSQ * 128]), ("vc", [128, NSQ * 128]), ("vc2", [128, NSQ * 128]),
                        ("kraw", [NSQ * 128, 128]), ("vraw", [NSQ * 128, 128])]:
            I[nm] = self.dram(nm, shp, F32, "ExternalInput")
        O = {}
        for nm, shp in [("yT", [D, NOWN]), ("conv_p", [128, 72]), ("s_p", [128, 1024]), ("k_p", [128, 128]),
                        ("v_p", [128, 128]), ("conv_s", [128, 24 * NSQ * 3]), ("s_s", [128, NSQ * 1024]),
                        ("kh_s", [NSQ * 64, 128]), ("vh_s", [NSQ * 64, 128]), ("kn_s", [128, NSA]),
                        ("vn_s", [128, 2 * 128])]:
            O[nm] = self.dram(nm, shp, F32, "ExternalOutput")
        self.I, self.O = I, O
        self.W = {}
        X = {}
        for nm, dt in [("x1", F32), ("o0", F32), ("oP", BF16), ("zg", BF16), ("sga", BF16), ("mb", BF16)]:
            X[nm] = self.dram("sp_" + nm, [D, NOWN], dt, "Internal")
        self.X = X
        ag_in = self.dram("ag_in", [128, 2048], F32, "Internal")
        ag_out = self.dram("ag_out", [1024, 2048], F32, "Internal")

        self.pst = [nc.alloc_psum_tensor("pst%d" % i, [128, 1024], F32) for i in range(4)]
        self.psb = [Buf("psb%d" % i) for i in range(8)]

        def wprep(name, src, nk, specs):
            scr = self.dram("bf_" + name, [len(specs) * 128, 4096], BF16, "Internal")
            srcv = src.h.rearrange("(k p) n -> p k n", p=128)
            tiles = []
            for j, pieces in enumerate(specs):
                ntot = sum(n for (_, n) in pieces)
                dv = scr.h[j * 128:(j + 1) * 128, 0:nk * ntot].rearrange("p (k n) -> p k n", k=nk)
                o = 0
                for (c0, n) in pieces:
                    S.dma(S.pool, "cast_" + name, dv[:, :, o:o + n], srcv[:, :, c0:c0 + n], reads=[src.b], writes=[scr.b])
                    o += n
                tiles.append(ntot)
            self.W[name] = (scr, nk, tiles)
        ffi = [[(gp * 256, 256), (DFF + gp * 256, 256)] for gp in range(HC // 2)]
        wprep("wada", I["wada"], KC, [[(j * 512, 512)] for j in range(18)])
        wprep("w1i", I["w1i"], KC, ffi)
        wprep("w1o", I["w1o"], HC, [[(m * 128, 128)] for m in range(8)])
        wprep("wsm", I["win"], KC, [[(C_B, 16), (C_KS, 128), (C_KS + 64, 64), (C_KS, 64), (C_VS, 128)]])
        for nm, c0, n in [("wqkv", C_QKV, 3072), ("wz", C_Z, 1024), ("wqs", C_QS, 1024), ("wga", C_GA, 1024), ("wgb", C_GB, 1024)]:
            wprep(nm, I["win"], KC, [[(c0 + j * 512, 512)] for j in range(n // 512)])
        wprep("wb", I["wb"], KC, [[(j * 512, 512)] for j in range(2)])
        wprep("wa", I["wa"], KC, [[(j * 512, 512)] for j in range(2)])
        wprep("wo", I["wo"], KC, [[(j * 512, 512)] for j in range(2)])
        wprep("w2i", I["w2i"], KC, ffi)
        wprep("w2o", I["w2o"], HC, [[(m * 128, 128)] for m in range(8)])

        cst = self.sb("cst", [128, NCST])
        S.dma(S.sp, "cst", cst.h[:], I["cst"].h[:, :], writes=[cst.b])
        self.cst = cst

        def C(name, a=0, b=None):
            o, w = CST[name]
            b = w if b is None else b
            return cst.h[:, o + a:o + b]
        self.C = C
        identb = self.sb("identb", [128, 128], BF16)
        onesb = self.sb("onesb", [128, 128], BF16)
        onesf = self.sb("onesf", [128, 128])
        self.cp(identb.h[:], C("ident"), [cst.b], [identb.b])
        self.memset(onesb.h[:], 1.0, [onesb.b])
        self.memset(onesf.h[:], 1.0, [onesf.b])
        self.identb, self.onesb, self.onesf = identb, onesb, onesf
        esk = self.sb("esk", [128, 16])
        S.dma(S.sp, "esk", esk.h[:], I["esk"].h[:, :], writes=[esk.b])
        self.act(esk.h[:], esk.h[:], AF.Exp, [esk.b], [esk.b])
        self.esk = esk
        nexpA = self.sb("nexpA", [128, 8])
        self.act(nexpA.h[:], C("alog"), AF.Exp, [cst.b], [nexpA.b])
        self.ts(nexpA.h[:], nexpA.h[:], -1.0, None, ALU.mult, None, [nexpA.b], [nexpA.b])
        self.nexpA = nexpA

        self.wsl = [self.sb("wsl%d" % i, [128, 4096], BF16) for i in range(3)]

        biasT = self.sb("biasT", [128, 3, 2, 512])
        self.biasT = biasT
        relb = self.sb("relb", [32, 16])
        S.dma(S.sp, "relb", relb.h[:], I["relb"].h[:, :], writes=[relb.b])
        self.xT = [self.sb("xT0", [128, KC, TS])]
        ohv = Tl(self.xT[0].h[0:32].rearrange("p k t -> p (k t)")[:, 0:1536], self.xT[0].b)
        ohs = [ohv, ohv]
        bps = []
        for b in range(3):
            t, c0, bufs = self.ps(2)
            bps.append((t, bufs))
        for ig in range(8):
            oht = ohs[ig % 2]
            S.dma(S.sp, "ohs", oht.h[:], I["oh"].h[:, ig * 1536:(ig + 1) * 1536], writes=[oht.b])
            for il in range(8):
                i = ig * 8 + il
                for b in range(3):
                    for hf in range(2):
                        t, bufs = bps[b]
                        self.mm(t[hf * 64:(hf + 1) * 64, i * 16:(i + 1) * 16],
                                oht.h[:, il * 192 + b * 64: il * 192 + (b + 1) * 64], relb.h[:, :], True, True,
                                [oht.b, relb.b], bufs)
        for b in range(3):
            t, bufs = bps[b]
            for kv in range(2):
                for par in range(2):
                    src = t[:, :].rearrange("p (q kv j par) -> p kv par j q", q=64, kv=2, j=4, par=2)[:, kv, par]
                    dst = biasT.h[:, b, kv, par * 256:(par + 1) * 256].rearrange("p (j q) -> p j q", j=4)
                    self.cp(dst, src, bufs, [biasT.b])

        scT = self.sb("scT", [128, KC, 5], BF16)
        self.act(scT.h[:], C("cT").rearrange("p (k m) -> p k m", k=KC), AF.Silu, [cst.b], [scT.b])
        mod = self.sb("mod", [128, 72, 5])
        t, c0, bufs = self.ps(1)
        for j in range(18):
            wt, ws = self.wtile("wada", j)
            for cc in range(4):
                col = j * 4 + cc
                for k in range(KC):
                    self.mm(t[:, c0 + col * 5: c0 + col * 5 + 5], wt[:, k, cc * 128:(cc + 1) * 128], scT.h[:, k, :],
                            k == 0, k == KC - 1, [ws.b, scT.b], bufs)
        self.tt(mod.h[:], t[:, c0:c0 + 360].rearrange("p (c m) -> p c m", m=5),
                C("bada")[:, :, None].to_broadcast([128, 72, 5]), ALU.add, bufs + [cst.b], [mod.b])
        self.modv = {}
        for s, nrm, gscale in [(1, "norm1", 0.5), (2, "norm2", 1.0), (3, "norm3", 0.5)]:
            A = self.sb("A%d" % s, [128, KC, 5])
            G = self.sb("G%d" % s, [128, KC, 5])
            sh = mod.h[:, (3 * s - 3) * 8:(3 * s - 2) * 8, :]
            sc = mod.h[:, (3 * s - 2) * 8:(3 * s - 1) * 8, :]
            ga = mod.h[:, (3 * s - 1) * 8:(3 * s) * 8, :]
            self.ts(A.h[:], sc, 1.0, None, ALU.add, None, [mod.b], [A.b])
            self.tt(A.h[:], A.h[:], C(nrm)[:, :, None].to_broadcast([128, KC, 5]), ALU.mult, [A.b, cst.b], [A.b])
            self.ts(G.h[:], ga, gscale, None, ALU.mult, None, [mod.b], [G.b])
            self.modv[s] = (A, sh, G)
        self.mod = mod

        T = TS
        self.hq = self.sb("hq", [128, KC, T], BF16)
        self.rs = self.sb("rs", [128, T])
        self.tmpc = [self.sb("tmpc%d" % i, [128, T]) for i in range(2)]
        self.qkv = self.sb("qkv", [128, 24, T], BF16)
        self.aT = Tl(self.qkv.h[:, 0:HC, :], self.qkv.b)
        self.sg = [self.sb("sg%d" % i, [128, T]) for i in range(2)]
        pre0 = self.sb("pre0", [128, T + 3 * NSQ])
        self.pre = [pre0, pre0]
        acc0 = self.sb("acc0", [128, T]); sil0 = self.sb("sil0", [128, T]); sqk0 = self.sb("sqk0", [128, T], BF16)
        self.acc = [acc0, acc0]
        self.sil = [sil0, sil0]
        self.sqk = [sqk0, sqk0]
        self.ztmp = [self.sb("ztmp%d" % i, [128, T], BF16) for i in range(2)]
        self.qs = self.sb("qs", [64, T // 64, 16, 64], BF16)
        self.sgb = self.sb("sgb", [128, KC, T], BF16)
        self.obT = self.sb("obT", [128, KC, T], BF16)
        self.ba = self.sb("ba", [128, T // 128, 16])
        self.carry = self.sb("carry", [128, 24, 3])
        self.memset(self.carry.h[:], 0.0, [self.carry.b])
        self.cvout = self.sb("cvout", [128, 24, NSQ, 3])
        self.convst = self.sb("convst", [128, 24, NSQ, 3])
        S.dma(S.sp, "convst", self.convst.h[:], I["convst"].h[:, :].rearrange("p (c s w) -> p c s w", c=24, s=NSQ),
              writes=[self.convst.b])
        stmp0 = self.sb("stmp0", [128, 512])
        self.stmp = [stmp0, stmp0]
        self.pT = self.sb("pT", [128, 3, 512], BF16)
        self.rden = self.sb("rden", [128, 512])
        self.ksA = self.sb("ksA", [128, HALO + NPR], BF16)
        self.ksB = self.sb("ksB", [128, HALO + NPR], BF16)
        self.vsb = self.sb("vsb", [128, (HALO + NPR) // 128, 128], BF16)
        self.ksAs = self.sb("ksAs", [128, NSA], BF16)
        self.ksBs = self.sb("ksBs", [128, NSA], BF16)
        self.vsbs = self.sb("vsbs", [128, NSA // 128, 128], BF16)
        self.vodd = self.sb("vodd", [64, (HALO + NPR) // 128, 128], BF16)
        self.vodds = self.sb("vodds", [64, NSA // 128, 128], BF16)
        self.vcs2 = self.sb("vcs2", [128, NSQ, 128], BF16)
        self.kcA = self.sb("kcA", [128, NSQ, 128], BF16)
        self.kcB = self.sb("kcB", [128, NSQ, 128], BF16)
        self.vcs = self.sb("vcs", [128, NSQ, 128], BF16)
        ktmp = self.stmp[0]
        for nm, dst in [("kcA", self.kcA), ("kcB", self.kcB), ("vc", self.vcs), ("vc2", self.vcs2)]:
            S.dma(S.sp, "ktmp", ktmp.h[:], I[nm].h[:, :], writes=[ktmp.b])
            self.cp(dst.h[:].rearrange("p s k -> p (s k)"), ktmp.h[:], [ktmp.b], [dst.b])
        self.kout = self.sb("kout", [128, TS])
        self.vout = self.sb("vout", [128, TS // 128, 128])
        self.S32 = self.sb("S32", [128, 8, 256])
        self.Sb = self.sb("Sb", [128, 8, 256], BF16)
        self.memset(self.S32.h[:], 0.0, [self.S32.b])
        for h in range(8):
            self.cp(self.S32.h[:, h, 128:256], C("ident"), [cst.b], [self.S32.b])
        self.cp(self.Sb.h[:], self.S32.h[:], [self.S32.b], [self.Sb.b])
        self.Ss32 = self.sb("Ss32", [128, 8, 128])
        self.Ssb = self.sb("Ssb", [128, 8, 128], BF16)
        g = {}
        for nm, shp, dt in [("beta", [128, 8], F32), ("gt", [128, 8], F32), ("gg", [128, 8], F32), ("gccol", [128, 8], F32),
                            ("ebg", [128, 8], F32), ("ed", [128, 8], F32), ("egl", [128, 16], F32), ("bexpg", [128, 8], F32),
                            ("G8", [128, 8, 128], F32), ("dtmp", [128, 4, 128], F32), ("decay", [128, 4, 128], F32),
                            ("decayT", [128, 4, 128], F32), ("egr", [128, 4, 128], F32), ("qg", [128, 8, 128], BF16),
                            ("L", [128, 4, 128], F32), ("B", [128, 4, 128], F32), ("L2", [128, 4, 128], F32),
                            ("B2", [128, 4, 128], F32), ("Tt", [128, 4, 128], F32), ("TtB", [128, 8, 128], BF16),
                            ("intraT", [128, 8, 128], BF16), ("kbg", [128, 8, 128], BF16), ("kd", [128, 2, 8, 128], BF16), ("edm", [128, 2, 8], F32),
                            ("vb", [128, 8, 128], BF16), ("uext", [128, 8, 256], F32), ("wT", [128, 8, 128], BF16),
                            ("vn", [128, 8, 256], BF16), ("o0st", [128, 8, 128], F32), ("oPst", [128, 8, 128], BF16)]:
            g[nm] = self.sb("g_" + nm, shp, dt)
        self.memset(g["uext"].h[:], 0.0, [g["uext"].b])
        self.memset(g["vn"].h[:], 0.0, [g["vn"].b])
        self.g = g

        if self.stop == 1:
            return self.finish()
        self.passA("H", 0, HALO, xcol0=0)
        if self.stop == 2:
            return self.finish()
        self.ts(self.carry.h[:], self.carry.h[:], C("flag"), None, ALU.mult, None, [self.carry.b, cst.b], [self.carry.b])
        for st in range(NPR // TS):
            self.passA("P", st, TS, xcol0=HALO + st * TS)
            if self.stop == 3:
                return self.finish()
            self.chk(300 + st)
        S.dma(S.pool, "o_convp", O["conv_p"].h[:, :], self.carry.h[:].rearrange("p c w -> p (c w)"), reads=[self.carry.b], writes=[O["conv_p"].b])
        S.dma(S.pool, "agin", ag_in.h[:, :], self.S32.h[:].rearrange("p h w -> p (h w)"), reads=[self.S32.b], writes=[ag_in.b])
        S._wait(S.pool, S._deps([ag_in.b], [ag_out.b]))
        S.wait_all_dma(S.pool)
        ccs = nc.alloc_semaphore("ccs")
        S.sems[id(ccs)] = ccs
        nc.gpsimd.collective_compute("AllGather", ALU.bypass, replica_groups=[list(range(NCORES))],
                                     ins=[ag_in.h.opt()], outs=[ag_out.h.opt()]).then_inc(ccs)
        nc.gpsimd.wait_ge(ccs, 1)
        S.pool.seen[id(ccs)] = 1
        ag_out.b.lw = (id(ccs), 1)
        if self.stop == 4:
            return self.finish()
        self.passA("S", 0, NSA, xcol0=HALO + NPR)
        S.dma(S.pool, "o_convs", O["conv_s"].h[:, :], self.cvout.h[:].rearrange("p c s w -> p (c s w)"), reads=[self.cvout.b], writes=[O["conv_s"].b])
        for i in range(NSQ):
            S.dma(S.pool, "o_kh", O["kh_s"].h[i * 64:(i + 1) * 64, :], I["kraw"].h[i * 128 + 64:(i + 1) * 128, :], writes=[O["kh_s"].b])
            S.dma(S.pool, "o_vh", O["vh_s"].h[i * 64:(i + 1) * 64, :], I["vraw"].h[i * 128 + 64:(i + 1) * 128, :], writes=[O["vh_s"].b])

        if self.stop == 5:
            return self.finish()
        S0 = self.g["o0st"]
        S0b = self.g["oPst"]
        phT = self.g["G8"]
        ctmp = self.sb("ctmp", [128, 8, 128])
        agr = [self.g["uext"], self.g["uext"]]
        self.memset(S0.h[:], 0.0, [S0.b])

        def apply_comp(src, srcb, dst_fn):
            for hg in range(2):
                t, c0, bufs = self.ps(1)
                for hh in range(4):
                    h = hg * 4 + hh
                    self.mm(t[:, c0 + hh * 128: c0 + (hh + 1) * 128], src[:, h, 128:256], C("ident"), True, True, [srcb, cst.b], bufs)
                self.cp(phT.h[:, hg * 4:(hg + 1) * 4, :], t[:, c0:c0 + 512].rearrange("p (h d) -> p h d", h=4), bufs, [phT.b], eng=S.act)
                t2, c2, bufs2 = self.ps(1)
                for hh in range(4):
                    h = hg * 4 + hh
                    self.mm(t2[:, c2 + hh * 128: c2 + (hh + 1) * 128], phT.h[:, h, :], S0.h[:, h, :], True, True, [phT.b, S0.b], bufs2)
                self.tt(ctmp.h[:, hg * 4:(hg + 1) * 4, :], t2[:, c2:c2 + 512].rearrange("p (h d) -> p h d", h=4),
                        src[:, hg * 4:(hg + 1) * 4, 0:128], ALU.add, bufs2 + [srcb], [ctmp.b])
            dst_fn()

        for r in range(NCORES - 1):
            a = agr[r % 2]
            S.dma(S.pool, "agr", a.h[:].rearrange("p h w -> p (h w)"), ag_out.h[r * 128:(r + 1) * 128, :],
                  reads=[ag_out.b], writes=[a.b])

            def upd(r=r):
                self.tt(ctmp.h[:], ctmp.h[:], S0.h[:], ALU.subtract, [ctmp.b, S0.b], [ctmp.b])
                self.stt(S0.h[:], ctmp.h[:], C("rankmask", r, r + 1), S0.h[:], ALU.mult, ALU.add, [ctmp.b, S0.b, cst.b], [S0.b])
            apply_comp(a.h, a.b, upd)
        self.cp(S0b.h[:], S0.h[:], [S0.b], [S0b.b])
        self.S0b = S0b
        apply_comp(self.S32.h, self.S32.b, lambda: None)
        S.dma(S.pool, "o_sp", O["s_p"].h[:, :], ctmp.h[:].rearrange("p h d -> p (h d)"), reads=[ctmp.b], writes=[O["s_p"].b])

        if self.stop == 6:
            return self.finish()
        self.o0b = self.g["uext"]
        self.oPb = Tl(self.qkv.h[:, 0:8, :], self.qkv.b)
        self.zgb = Tl(self.qkv.h[:, 8:16, :], self.qkv.b)
        self.sgab = Tl(self.qkv.h[:, 16:24, :], self.qkv.b)
        self.mbb = Tl(self.vsb.h[:].rearrange("p a d -> p (a d)")[:, 0:KC * T].rearrange("p (k t) -> p k t", k=KC), self.vsb.b)
        self.onT = self.sgb
        self.mgT = self.obT
        for st in range(NPR // TS):
            self.passB("P", st * TS, TS)
            if self.stop == 7:
                return self.finish()
        self.passB("S", NPR, NSA)
        return self.finish()

    def segs(self, kind, T):
        if kind == "S":
            return [(i * LSQ, LSQ, 1 + i) for i in range(NSQ)]
        return [(0, T, 0)]

    def norm_mod(self, x, T, segs, A, sh, hout):
        S, C = self.S, self.C
        sq = self.hq
        self.act(sq.h[:, :, 0:T], x.h[:, :, 0:T], AF.Square, [x.b], [sq.b])
        t, c0, bufs = self.ps(1)
        for k in range(KC):
            self.mm(t[:, c0:c0 + T], self.onesb.h[:, :], sq.h[:, k, 0:T], k == 0, k == KC - 1, [self.onesb.b, sq.b], bufs)
        rs = self.rs
        self.act(rs.h[:, 0:T], t[:, c0:c0 + T], AF.Sqrt, bufs + [self.cst.b], [rs.b], bias=C("eps"), scale=1.0 / D)
        self.recip(rs.h[:, 0:T], rs.h[:, 0:T], [rs.b], [rs.b])
        for c in range(KC):
            tc_ = self.tmpc[c % 2]
            self.tt(tc_.h[:, 0:T], x.h[:, c, 0:T], rs.h[:, 0:T], ALU.mult, [x.b, rs.b], [tc_.b])
            for (s0, sl, mi) in segs:
                if sh is None:
                    self.act(hout.h[:, c, s0:s0 + sl], tc_.h[:, s0:s0 + sl], AF.Identity, [tc_.b, A.b], [hout.b], scale=A.h[:, c:c + 1])
                else:
                    self.act(hout.h[:, c, s0:s0 + sl], tc_.h[:, s0:s0 + sl], AF.Identity, [tc_.b, A.b, self.mod.b], [hout.b],
                             bias=sh[:, c, mi:mi + 1], scale=A.h[:, c, mi:mi + 1])

    def ffn(self, x, T, segs, s, wi, wo):
        S = self.S
        A, sh, G = self.modv[s]
        h = self.hq
        self.norm_mod(x, T, segs, A, sh, h)
        for gp in range(HC // 2):
            wt, ws = self.wtile(wi, gp)
            for e in range(2):
                i = gp * 2 + e
                t, c0, bufs = self.ps(1)
                for k in range(KC):
                    self.mm(t[:, c0:c0 + T], wt[:, k, e * 128:(e + 1) * 128], h.h[:, k, 0:T], k == 0, k == KC - 1, [ws.b, h.b], bufs)
                for k in range(KC):
                    self.mm(t[:, c0 + 256:c0 + 256 + T], wt[:, k, 256 + e * 128:256 + (e + 1) * 128], h.h[:, k, 0:T], k == 0, k == KC - 1,
                            [ws.b, h.b], bufs)
                sg = self.sg[i % 2]
                self.act(sg.h[:, 0:T], t[:, c0:c0 + T], AF.Silu, bufs, [sg.b])
                self.tt(self.aT.h[:, i, 0:T], sg.h[:, 0:T], t[:, c0 + 256:c0 + 256 + T], ALU.mult, [sg.b] + bufs, [self.aT.b])
        for m in range(8):
            wt, ws = self.wtile(wo, m)
            t, c0, bufs = self.ps(1)
            for k in range(HC):
                self.mm(t[:, c0:c0 + T], wt[:, k, :], self.aT.h[:, k, 0:T], k == 0, k == HC - 1, [ws.b, self.aT.b], bufs)
            for (s0, sl, mi) in segs:
                self.stt(x.h[:, m, s0:s0 + sl], t[:, c0 + s0:c0 + s0 + sl], G.h[:, m, mi:mi + 1], x.h[:, m, s0:s0 + sl],
                         ALU.mult, ALU.add, bufs + [G.b, x.b], [x.b])

    def proj(self, wname, rhs, T, fn):
        scr, nk, tiles = self.W[wname]
        c = 0
        for j, n in enumerate(tiles):
            wt, ws = self.wtile(wname, j)
            for cc in range(n // 128):
                t, c0, bufs = self.ps(1)
                for k in range(KC):
                    self.mm(t[:, c0:c0 + T], wt[:, k, cc * 128:(cc + 1) * 128], rhs.h[:, k, 0:T], k == 0, k == KC - 1, [ws.b, rhs.b], bufs)
                fn(c, t, c0, bufs)
                c += 1

    def passA(self, kind, st, T, xcol0):
        S, C, I, W, X = self.S, self.C, self.I, self.W, self.X
        segs = self.segs(kind, T)
        x = self.xT[0]
        xv = I["xT"].h.rearrange("(k p) t -> p k t", p=128)
        S.dma(S.pool, "x_" + x.b.name, x.h[:, :, 0:T], xv[:, :, xcol0:xcol0 + T], writes=[x.b])
        self.ffn(x, T, segs, 1, "w1i", "w1o")
        own0 = None
        if kind == "P":
            own0 = st * TS
        elif kind == "S":
            own0 = NPR
        if own0 is not None:
            S.dma(S.pool, "sp_x1", X["x1"].h.rearrange("(k p) t -> p k t", p=128)[:, :, own0:own0 + T], x.h[:, :, 0:T],
                  reads=[x.b], writes=[X["x1"].b])
        A2, sh2, G2 = self.modv[2]
        h2 = self.hq
        self.norm_mod(x, T, segs, A2, sh2, h2)
        nseg = len(segs)
        L = T // nseg
        wsm_v, wsm_s = self.wtile("wsm", 0)
        if kind == "S":
            ksA, ksB, vsb, vodd, kc0, vt0 = self.ksAs, self.ksBs, self.vsbs, self.vodds, 0, 0
        else:
            ksA, ksB, vsb, vodd = self.ksA, self.ksB, self.vsb, self.vodd
            kc0 = 0 if kind == "H" else HALO + st * TS
            vt0 = kc0 // 128
        last = (kind == "S") or (kind == "P" and st == NPR // TS - 1)
        if _os.environ.get("NOLAST") == "1":
            last = False
        for gi, (dst, wc) in enumerate([(ksA, 16), (ksB, 144)]):
            t, c0, bufs = self.ps(1)
            for k in range(KC):
                self.mm(t[:, c0:c0 + T], wsm_v[:, k, wc:wc + 128], h2.h[:, k, 0:T], k == 0, k == KC - 1, [wsm_s.b, h2.b], bufs)
            self.cp(dst.h[:, kc0:kc0 + T], t[:, c0:c0 + T], bufs, [dst.b], eng=S.act)
            if gi == 0 and last:
                self.cp(self.kout.h[:, 0:T], t[:, c0:c0 + T], bufs + [dst.b], [self.kout.b])
        for tt_ in range(T // 128):
            t, c0, bufs = self.ps(1)
            for k in range(KC):
                self.mm(t[:, c0:c0 + 128], h2.h[:, k, tt_ * 128:(tt_ + 1) * 128], wsm_v[:, k, 272:400], k == 0, k == KC - 1,
                        [wsm_s.b, h2.b], bufs)
            self.cp(vsb.h[:, vt0 + tt_, :], t[:, c0:c0 + 128], bufs, [vsb.b], eng=S.act)
            if last:
                self.cp(self.vout.h[:, tt_, :], t[:, c0:c0 + 128], bufs + [vsb.b], [self.vout.b])
            t, c0, bufs2 = self.ps(1)
            for k in range(KC):
                self.mm(t[0:64, c0:c0 + 128], h2.h[:, k, tt_ * 128 + 64:(tt_ + 1) * 128], wsm_v[:, k, 272:400], k == 0, k == KC - 1,
                        [wsm_s.b, h2.b], bufs2)
            self.cp(vodd.h[0:64, vt0 + tt_, :], t[0:64, c0:c0 + 128], bufs2, [vodd.b], eng=S.act)
            if kind != "H":
                t, c0, bufs = self.ps(1)
                for k in range(KC):
                    self.mm(t[:, c0:c0 + 16], h2.h[:, k, tt_ * 128:(tt_ + 1) * 128], wsm_v[:, k, 0:16], k == 0, k == KC - 1,
                            [wsm_s.b, h2.b], bufs)
                self.cp(self.ba.h[:, tt_, :], t[:, c0:c0 + 16], bufs, [self.ba.b], eng=S.act)
        if last:
            O = self.O
            if kind == "P":
                S.dma(S.pool, "o_kp", O["k_p"].h[:, :], self.kout.h[:, T - 128:T], reads=[self.kout.b], writes=[O["k_p"].b])
                S.dma(S.pool, "o_vp", O["v_p"].h[:, :], self.vout.h[:, T // 128 - 1, :], reads=[self.vout.b], writes=[O["v_p"].b])
            else:
                S.dma(S.pool, "o_kns", O["kn_s"].h[:, :], self.kout.h[:, 0:T], reads=[self.kout.b], writes=[O["kn_s"].b])
                S.dma(S.pool, "o_vns", O["vn_s"].h[:, :], self.vout.h[:, 0:T // 128, :].rearrange("p a d -> p (a d)"),
                      reads=[self.vout.b], writes=[O["vn_s"].b])

        cw = C("convw").rearrange("p (c w) -> p c w", c=24)

        def f_qkv(cc, t, c0, bufs):
            pre = self.pre[cc % 2]
            pv = pre.h[:, 0:nseg * (L + 3)].rearrange("p (s l) -> p s l", s=nseg)
            if kind == "S":
                self.cp(pv[:, :, 0:3], self.convst.h[:, cc, :, :], [self.convst.b], [pre.b], eng=S.pool)
            else:
                self.cp(pv[:, 0, 0:3], self.carry.h[:, cc, :], [self.carry.b], [pre.b], eng=S.pool)
            self.cp(pv[:, :, 3:3 + L], t[:, c0:c0 + T].rearrange("p (s l) -> p s l", s=nseg), bufs, [pre.b], eng=S.act)
            if kind == "S":
                self.cp(self.cvout.h[:, cc, :, :], pv[:, :, L:L + 3], [pre.b], [self.cvout.b], eng=S.pool)
            else:
                self.cp(self.carry.h[:, cc, :], pv[:, 0, L:L + 3], [pre.b], [self.carry.b], eng=S.pool)
            if kind == "H":
                return
            acc = self.acc[cc % 2]
            av = acc.h[:, 0:T].rearrange("p (s l) -> p s l", s=nseg)
            self.ts(av, pv[:, :, 0:L], cw[:, cc, 0:1], None, ALU.mult, None, [pre.b, self.cst.b], [acc.b])
            for w in range(1, 4):
                self.stt(av, pv[:, :, w:w + L], cw[:, cc, w:w + 1], av, ALU.mult, ALU.add, [pre.b, self.cst.b, acc.b], [acc.b])
            if cc >= 16:
                self.act(self.qkv.h[:, cc, 0:T], acc.h[:, 0:T], AF.Silu, [acc.b], [self.qkv.b])
                return
            sil = self.sil[cc % 2]
            self.act(sil.h[:, 0:T], acc.h[:, 0:T], AF.Silu, [acc.b], [sil.b])
            sqk = self.sqk[cc % 2]
            self.act(sqk.h[:, 0:T], sil.h[:, 0:T], AF.Square, [sil.b], [sqk.b])
            t2, c2, bufs2 = self.ps(1)
            self.mm(t2[:, c2:c2 + T], self.onesb.h[:, :], sqk.h[:, 0:T], True, True, [self.onesb.b, sqk.b], bufs2)
            rr = self.tmpc[cc % 2]
            if cc < 8:
                self.act(rr.h[:, 0:T], t2[:, c2:c2 + T], AF.Sqrt, bufs2 + [self.cst.b], [rr.b], bias=C("eps128"), scale=128.0)
            else:
                self.act(rr.h[:, 0:T], t2[:, c2:c2 + T], AF.Sqrt, bufs2 + [self.cst.b], [rr.b], bias=C("eps"), scale=1.0)
            self.recip(rr.h[:, 0:T], rr.h[:, 0:T], [rr.b], [rr.b])
            self.tt(self.qkv.h[:, cc, 0:T], sil.h[:, 0:T], rr.h[:, 0:T], ALU.mult, [sil.b, rr.b], [self.qkv.b])
        self.proj("wqkv", h2, T, f_qkv)
        if kind == "H":
            return
        self.chk(31)

        def spill(name, src, c):
            S.dma(S.pool, "sp_" + name + src.b.name, X[name].h.rearrange("(k p) t -> p k t", p=128)[:, c, own0:own0 + T], src.h[:, 0:T],
                  reads=[src.b], writes=[X[name].b])

        def f_z(c, t, c0, bufs):
            z = self.ztmp[c % 2]
            self.act(z.h[:, 0:T], t[:, c0:c0 + T], AF.Silu, bufs, [z.b])
            spill("zg", z, c)
        self.proj("wz", h2, T, f_z)

        for j in range(2):
            wt, ws = self.wtile("wqs", j)
            for hh in range(8):
                head = j * 8 + hh
                kvh, gq = head // 8, head % 8
                hidx = kvh * 8 + (gq % 2) * 4 + gq // 2
                t, c0, bufs = self.ps(1)
                for k in range(KC):
                    self.mm(t[0:64, c0:c0 + T], wt[:, k, hh * 64:(hh + 1) * 64], h2.h[:, k, 0:T], k == 0, k == KC - 1, [ws.b, h2.b], bufs)
                self.act(self.qs.h[0:64, 0:T // 64, hidx, :], t[0:64, c0:c0 + T].rearrange("p (a q) -> p a q", q=64), AF.Identity, bufs,
                         [self.qs.b], scale=0.125)

        def f_ga(c, t, c0, bufs):
            z = self.ztmp[c % 2]
            self.act(z.h[:, 0:T], t[:, c0:c0 + T], AF.Sigmoid, bufs, [z.b])
            spill("sga", z, c)
        self.proj("wga", h2, T, f_ga)

        def f_gb(c, t, c0, bufs):
            self.act(self.sgb.h[:, c, 0:T], t[:, c0:c0 + T], AF.Sigmoid, bufs, [self.sgb.b])
        self.proj("wgb", h2, T, f_gb)
        self.chk(32)

        for tt_ in range(T // 128):
            self.gdn_tile(kind, st, tt_, own0)
            self.chk(33)
            self.swa_tile(kind, st, tt_)
            self.chk(34)
        def f_wb(c, t, c0, bufs):
            z = self.ztmp[c % 2]
            self.tt(z.h[:, 0:T], t[:, c0:c0 + T], self.sgb.h[:, c, 0:T], ALU.mult, bufs + [self.sgb.b], [z.b])
            spill("mb", z, c)
        self.proj("wb", self.obT, T, f_wb)

    def gdn_tile(self, kind, st, tt_, own0):
        S, C, g, X = self.S, self.C, self.g, self.X
        cst = self.cst
        tc0 = tt_ * 128
        braw = self.ba.h[:, tt_, 0:8]
        araw = self.ba.h[:, tt_, 8:16]
        self.act(g["beta"].h[:], braw, AF.Sigmoid, [self.ba.b], [g["beta"].b])
        self.tt(g["gt"].h[:], araw, C("dtb"), ALU.add, [self.ba.b, cst.b], [g["gt"].b])
        self.act(g["gt"].h[:], g["gt"].h[:], AF.Exp, [g["gt"].b], [g["gt"].b])
        self.act(g["gt"].h[:], g["gt"].h[:], AF.Ln, [g["gt"].b, cst.b], [g["gt"].b], bias=C("one"), scale=1.0)
        self.tt(g["gg"].h[:], g["gt"].h[:], self.nexpA.h[:], ALU.mult, [g["gt"].b, self.nexpA.b], [g["gg"].b])
        gg = g["gg"]
        t, c0, bufs = self.ps(1)
        self.mm(t[:, c0:c0 + 8], C("ublk"), gg.h[:], True, True, [cst.b, gg.b], bufs)
        self.mm(t[:, c0 + 8:c0 + 16], C("blk"), gg.h[:], True, True, [cst.b, gg.b], bufs)
        ob = C("onesblk")
        self.mm(t[:, c0 + 16:c0 + 24], ob[:, 0:128], gg.h[:], True, True, [cst.b, gg.b], bufs)
        self.mm(t[:, c0 + 24:c0 + 32], ob[:, 128:256], gg.h[:], True, True, [cst.b, gg.b], bufs)
        self.cp(g["egl"].h[:], t[:, c0 + 16:c0 + 32], bufs, [g["egl"].b])
        self.cp(g["gccol"].h[:], t[:, c0:c0 + 8], bufs, [g["gccol"].b])
        self.tt(g["ed"].h[:], t[:, c0 + 8:c0 + 16], g["gccol"].h[:], ALU.subtract, bufs + [g["gccol"].b], [g["ed"].b])
        self.act(g["ebg"].h[:], g["gccol"].h[:], AF.Exp, [g["gccol"].b], [g["ebg"].b])
        self.act(g["ed"].h[:], g["ed"].h[:], AF.Exp, [g["ed"].b], [g["ed"].b])
        self.act(g["egl"].h[:], g["egl"].h[:], AF.Exp, [g["egl"].b], [g["egl"].b])
        self.tt(g["bexpg"].h[:], g["beta"].h[:], g["ebg"].h[:], ALU.mult, [g["beta"].b, g["ebg"].b], [g["bexpg"].b])
        for cc_ in range(2):
            self.ts(g["edm"].h[:, cc_, :], g["ed"].h[:], C("onesblk", cc_ * 128, cc_ * 128 + 1), None, ALU.mult, None,
                    [g["ed"].b, cst.b], [g["edm"].b])
        self.chk(331)
        self.tt(g["G8"].h[:], C("ublk")[:, None, :].to_broadcast([128, 8, 128]), gg.h[:, :, None].to_broadcast([128, 8, 128]),
                ALU.mult, [cst.b, gg.b], [g["G8"].b])
        qT = self.qkv
        for hg in range(2):
            hs = slice(hg * 4, hg * 4 + 4)
            tr, cr, bufr = self.ps(1)
            self.mm(tr[:, cr:cr + 512], self.onesf.h[:, :], g["G8"].h[:, hs, :].rearrange("p h i -> p (h i)"), True, True,
                    [self.onesf.b, g["G8"].b], bufr)
            gcrow = tr[:, cr:cr + 512].rearrange("p (h i) -> p h i", h=4)
            gccb = g["gccol"].h[:, hs, None].to_broadcast([128, 4, 128])
            dt = g["dtmp"]
            self.stt(dt.h[:], gcrow, -1.0, gccb, ALU.mult, ALU.add, bufr + [g["gccol"].b], [dt.b])
            self.tt(dt.h[:], dt.h[:], C("maskl")[:, None, :].to_broadcast([128, 4, 128]), ALU.add, [dt.b, cst.b], [dt.b])
            self.act(g["decay"].h[:], dt.h[:], AF.Exp, [dt.b], [g["decay"].b])
            self.tt(dt.h[:], gcrow, gccb, ALU.subtract, bufr + [g["gccol"].b], [dt.b])
            self.tt(dt.h[:], dt.h[:], C("masku")[:, None, :].to_broadcast([128, 4, 128]), ALU.add, [dt.b, cst.b], [dt.b])
            self.act(g["decayT"].h[:], dt.h[:], AF.Exp, [dt.b], [g["decayT"].b])
            self.act(g["egr"].h[:], gcrow, AF.Exp, bufr + [dt.b], [g["egr"].b])
            self.tt(g["qg"].h[:, hs, :], qT.h[:, hs, tc0:tc0 + 128], g["egr"].h[:], ALU.mult, [qT.b, g["egr"].b], [g["qg"].b])
            self.chk(332)
            tk, ck, bufk = self.ps(1)
            tq, cq, bufq = self.ps(1)
            for hh in range(4):
                h = hg * 4 + hh
                kT = qT.h[:, 8 + h, tc0:tc0 + 128]
                self.mm(tk[:, ck + hh * 128:ck + (hh + 1) * 128], kT, kT, True, True, [qT.b], bufk)
                self.mm(tq[:, cq + hh * 128:cq + (hh + 1) * 128], kT, qT.h[:, h, tc0:tc0 + 128], True, True, [qT.b], bufq)
            Lm, Bm, L2, B2, Tt = g["L"], g["B"], g["L2"], g["B2"], g["Tt"]
            self.tt(Lm.h[:], tk[:, ck:ck + 512].rearrange("p (h i) -> p h i", h=4), g["decay"].h[:], ALU.mult, bufk + [g["decay"].b], [Lm.b])
            self.tt(Lm.h[:], Lm.h[:], g["beta"].h[:, hs, None].to_broadcast([128, 4, 128]), ALU.mult, [Lm.b, g["beta"].b], [Lm.b])
            self.tt(g["intraT"].h[:, hs, :], tq[:, cq:cq + 512].rearrange("p (h i) -> p h i", h=4), g["decayT"].h[:], ALU.mult,
                    bufq + [g["decayT"].b], [g["intraT"].b])
            self.chk(333)
            tb, cb, bufb = self.ps(1)
            for hh in range(4):
                self.mm(tb[:, cb + hh * 128:cb + (hh + 1) * 128], Lm.h[:, hh, :], C("ident"), True, True, [Lm.b, cst.b], bufb)
            bview = tb[:, cb:cb + 512].rearrange("p (h i) -> p h i", h=4)
            self.chk(33301)
            self.cp(Bm.h[:], bview, bufb, [Bm.b], eng=S.act)
            self.chk(33302)
            self.stt(Tt.h[:], Bm.h[:], -1.0, C("ident")[:, None, :].to_broadcast([128, 4, 128]), ALU.mult, ALU.add, [cst.b, Bm.b], [Tt.b])
            self.chk(3331)
            Lc, Bc, Ln, Bn = Lm, Bm, L2, B2
            for lvl in range(5):
                t1, c1, buf1 = self.ps(1)
                for hh in range(4):
                    self.mm(t1[:, c1 + hh * 128:c1 + (hh + 1) * 128], Bc.h[:, hh, :], Lc.h[:, hh, :], True, True, [Bc.b, Lc.b], buf1)
                self.cp(Ln.h[:], t1[:, c1:c1 + 512].rearrange("p (h i) -> p h i", h=4), buf1, [Ln.b], eng=S.act)
                if lvl < 4:
                    t2, c2, buf2 = self.ps(1)
                    for hh in range(4):
                        self.mm(t2[:, c2 + hh * 128:c2 + (hh + 1) * 128], Lc.h[:, hh, :], Bc.h[:, hh, :], True, True, [Bc.b, Lc.b], buf2)
                    self.cp(Bn.h[:], t2[:, c2:c2 + 512].rearrange("p (h i) -> p h i", h=4), buf2, [Bn.b])
                t3, c3, buf3 = self.ps(1)
                for hh in range(4):
                    self.mm(t3[:, c3 + hh * 128:c3 + (hh + 1) * 128], Ln.h[:, hh, :], Tt.h[:, hh, :], True, True, [Ln.b, Tt.b], buf3)
                self.tt(Tt.h[:], Tt.h[:], t3[:, c3:c3 + 512].rearrange("p (h i) -> p h i", h=4), ALU.add, [Tt.b] + buf3, [Tt.b])
                Lc, Bc, Ln, Bn = Ln, Bn, Lc, Bc
                self.chk(3340 + lvl)
            self.cp(g["TtB"].h[:, hs, :], Tt.h[:], [Tt.b], [g["TtB"].b], eng=S.act)
            self.chk(334)
            tkk, ckk, bufkk = self.ps(1)
            tvv, cvv, bufvv = self.ps(1)
            for hh in range(4):
                h = hg * 4 + hh
                self.mm(tkk[:, ckk + hh * 128:ckk + (hh + 1) * 128], qT.h[:, 8 + h, tc0:tc0 + 128], self.identb.h[:, :], True, True,
                        [qT.b, self.identb.b], bufkk)
                self.mm(tvv[:, cvv + hh * 128:cvv + (hh + 1) * 128], qT.h[:, 16 + h, tc0:tc0 + 128], self.identb.h[:, :], True, True,
                        [qT.b, self.identb.b], bufvv)
            kview = tkk[:, ckk:ckk + 512].rearrange("p (h i) -> p h i", h=4)
            vview = tvv[:, cvv:cvv + 512].rearrange("p (h i) -> p h i", h=4)
            self.tt(g["kbg"].h[:, hs, :], kview, g["bexpg"].h[:, hs, None].to_broadcast([128, 4, 128]), ALU.mult, bufkk + [g["bexpg"].b], [g["kbg"].b])
            for cc_ in range(2):
                self.tt(g["kd"].h[:, cc_, hs, :], kview, g["edm"].h[:, cc_, hs, None].to_broadcast([128, 4, 128]), ALU.mult,
                        bufkk + [g["edm"].b], [g["kd"].b])
            self.tt(g["vb"].h[:, hs, :], vview, g["beta"].h[:, hs, None].to_broadcast([128, 4, 128]), ALU.mult, bufvv + [g["beta"].b], [g["vb"].b])
            tu, cu, bufu = self.ps(1)
            tw, cw_, bufw = self.ps(1)
            for hh in range(4):
                h = hg * 4 + hh
                self.mm(tu[:, cu + hh * 128:cu + (hh + 1) * 128], g["TtB"].h[:, h, :], g["vb"].h[:, h, :], True, True, [g["TtB"].b, g["vb"].b], bufu)
                self.mm(tw[:, cw_ + hh * 128:cw_ + (hh + 1) * 128], g["kbg"].h[:, h, :], g["TtB"].h[:, h, :], True, True, [g["TtB"].b, g["kbg"].b], bufw)
            self.cp(g["uext"].h[:, hs, 0:128], tu[:, cu:cu + 512].rearrange("p (h i) -> p h i", h=4), bufu, [g["uext"].b], eng=S.act)
            self.cp(g["wT"].h[:, hs, :], tw[:, cw_:cw_ + 512].rearrange("p (h i) -> p h i", h=4), bufw, [g["wT"].b])
        self.chk(335)
        for c in range(2):
            rows = slice(c * 64, c * 64 + 64)
            cols = slice(c * 64, c * 64 + 64)
            if kind == "S":
                sq = tt_ * 2 + c
                S32, Sb, Wd = self.Ss32, self.Ssb, 128
                S.dma(S.pool, "s0ld", S32.h[:], self.I["s0"].h[:, sq * 1024:(sq + 1) * 1024].rearrange("p (h d) -> p h d", h=8),
                      writes=[S32.b])
                self.cp(Sb.h[:], S32.h[:], [S32.b], [Sb.b])
            else:
                S32, Sb, Wd = self.S32, self.Sb, 256
            nbk = Wd // 128
            hpg = 512 // Wd * 2 if False else (4 if Wd == 256 else 8)
            for hg0 in range(0, 8, hpg):
                tv, cv, bufv = self.ps(2)
                for hh in range(hpg):
                    h = hg0 + hh
                    self.mm(tv[rows, hh * Wd:(hh + 1) * Wd], g["wT"].h[:, h, cols], Sb.h[:, h, 0:Wd], True, True, [g["wT"].b, Sb.b], bufv)
                self.tt(g["vn"].h[rows, hg0:hg0 + hpg, 0:Wd], g["uext"].h[rows, hg0:hg0 + hpg, 0:Wd],
                        tv[rows, 0:hpg * Wd].rearrange("p (h w) -> p h w", h=hpg), ALU.subtract, [g["uext"].b] + bufv, [g["vn"].b])
            self.chk(336)
            to, co, bufo = self.ps(2)
            for h in range(8):
                for bk in range(nbk):
                    oc = (bk * 8 + h) * 64
                    self.mm(to[:, oc:oc + 64], Sb.h[:, h, bk * 128:(bk + 1) * 128], g["qg"].h[:, h, cols], True, False, [Sb.b, g["qg"].b], bufo)
                    self.mm(to[:, oc:oc + 64], g["vn"].h[:, h, bk * 128:(bk + 1) * 128], g["intraT"].h[:, h, cols], False, True,
                            [g["vn"].b, g["intraT"].b], bufo)
            self.cp(g["o0st"].h[:, :, cols], to[:, 0:512].rearrange("p (h i) -> p h i", h=8), bufo, [g["o0st"].b], eng=S.act)
            if nbk == 2:
                self.cp(g["oPst"].h[:, :, cols], to[:, 512:1024].rearrange("p (h i) -> p h i", h=8), bufo, [g["oPst"].b], eng=S.act)
            self.chk(337)
            for hg0 in range(0, 8, hpg):
                tsn, csn, bufs_ = self.ps(2)
                for hh in range(hpg):
                    h = hg0 + hh
                    self.mm(tsn[:, hh * Wd:(hh + 1) * Wd], g["kd"].h[:, c, h, :], g["vn"].h[:, h, 0:Wd], True, True, [g["kd"].b, g["vn"].b], bufs_)
                for hh in range(hpg):
                    h = hg0 + hh
                    self.stt(S32.h[:, h, 0:Wd], S32.h[:, h, 0:Wd], g["egl"].h[:, c * 8 + h:c * 8 + h + 1], tsn[:, hh * Wd:(hh + 1) * Wd],
                             ALU.mult, ALU.add, [S32.b, g["egl"].b] + bufs_, [S32.b])
            self.cp(Sb.h[:, :, 0:Wd], S32.h[:, :, 0:Wd], [S32.b], [Sb.b], eng=S.act)
            if kind == "S":
                S.dma(S.pool, "o_ss", self.O["s_s"].h[:, sq * 1024:(sq + 1) * 1024], S32.h[:].rearrange("p h d -> p (h d)"),
                      reads=[S32.b], writes=[self.O["s_s"].b])
        oc0 = own0 + tc0
        S.dma(S.pool, "sp_o0", X["o0"].h.rearrange("(k p) t -> p k t", p=128)[:, :, oc0:oc0 + 128], g["o0st"].h[:],
              reads=[g["o0st"].b], writes=[X["o0"].b])
        if kind == "P":
            S.dma(S.pool, "sp_oP", X["oP"].h.rearrange("(k p) t -> p k t", p=128)[:, :, oc0:oc0 + 128], g["oPst"].h[:],
                  reads=[g["oPst"].b], writes=[X["oP"].b])

    def swa_tile(self, kind, st, tt_):
        S, C = self.S, self.C
        R = slice(0, 64)
        for c in range(2):
            qcols = slice(tt_ * 128 + c * 64, tt_ * 128 + c * 64 + 64)
            blocks = []
            if kind == "S":
                sq = tt_ * 2 + c
                blocks.append((self.kcA.h[:, sq, 0:64], self.kcB.h[:, sq, 0:64], [self.kcA.b, self.kcB.b], self.vcs.h[R, sq, :], self.vcs.b, False))
                blocks.append((self.kcA.h[:, sq, 64:128], self.kcB.h[:, sq, 64:128], [self.kcA.b, self.kcB.b], self.vcs2.h[R, sq, :], self.vcs2.b, False))
                vown = self.vsbs if c == 0 else self.vodds
                blocks.append((self.ksAs.h[:, sq * 64:(sq + 1) * 64], self.ksBs.h[:, sq * 64:(sq + 1) * 64], [self.ksAs.b, self.ksBs.b],
                               vown.h[R, tt_, :], vown.b, False))
            else:
                gci = (st * TS + tt_ * 128) // 64 + c
                for b in range(3):
                    tk = gci * 64 + b * 64
                    vsrc = self.vsb if (tk // 64) % 2 == 0 else self.vodd
                    blocks.append((self.ksA.h[:, tk:tk + 64], self.ksB.h[:, tk:tk + 64], [self.ksA.b, self.ksB.b],
                                   vsrc.h[R, tk // 128, :], vsrc.b, tk < HALO))
            for kv in range(2):
                tS, cS, bufS = self.ps(2)
                tS2, cS2, bufS2 = self.ps(1)
                regs = [(tS, 0, bufS[0:1]), (tS, 512, bufS[1:2]), (tS2, cS2, bufS2)]
                for b, (kA, kB, kbufs, vap, vbuf, masked) in enumerate(blocks):
                    t, c0, bufs = regs[b]
                    ksrc = kA if kv == 0 else kB
                    self.mm(t[R, c0:c0 + 512], ksrc[0:64, :],
                            self.qs.h[0:64, tt_ * 2 + c, kv * 8:kv * 8 + 8, :].rearrange("p j q -> p (j q)"),
                            True, True, kbufs + [self.qs.b], bufs)
                for b, (kA, kB, kbufs, vap, vbuf, masked) in enumerate(blocks):
                    t, c0, bufs = regs[b]
                    stp = self.stmp[b % 2]
                    self.tt(stp.h[R, :], t[R, c0:c0 + 512], self.biasT.h[R, b, kv, :], ALU.add, bufs + [self.biasT.b], [stp.b])
                    if masked:
                        self.act(self.pT.h[R, b, :], stp.h[R, :], AF.Exp, [stp.b, self.cst.b], [self.pT.b], bias=C("hmask")[R, :], scale=1.0)
                    else:
                        self.act(self.pT.h[R, b, :], stp.h[R, :], AF.Exp, [stp.b], [self.pT.b])
                tP, cP, bufP = self.ps(1)
                tD, cD, bufD = self.ps(1)
                for par in range(2):
                    for b, (kA, kB, kbufs, vap, vbuf, masked) in enumerate(blocks):
                        self.mm(tP[par * 64:(par + 1) * 64, cP:cP + 256], vap[:, kv * 64:(kv + 1) * 64],
                                self.pT.h[R, b, par * 256:(par + 1) * 256], b == 0, b == 2, [vbuf, self.pT.b], bufP)
                for b in range(3):
                    self.mm(tD[:, cD:cD + 512], self.onesb.h[R, :], self.pT.h[R, b, :], b == 0, b == 2, [self.onesb.b, self.pT.b], bufD)
                rd = self.rden
                self.tt(rd.h[:].rearrange("p (a q) -> p a q", a=8), tD[:, cD:cD + 512].rearrange("p (a q) -> p a q", a=8),
                        self.esk.h[:, kv * 8:(kv + 1) * 8, None].to_broadcast([128, 8, 64]), ALU.add, bufD + [self.esk.b], [rd.b])
                self.recip(rd.h[:], rd.h[:], [rd.b], [rd.b])
                for par in range(2):
                    pr = slice(par * 64, par * 64 + 64)
                    self.tt(self.obT.h[pr, kv * 4:kv * 4 + 4, qcols], tP[pr, cP:cP + 256].rearrange("p (j q) -> p j q", j=4),
                            rd.h[pr, par * 256:(par + 1) * 256].rearrange("p (j q) -> p j q", j=4), ALU.mult, bufP + [rd.b], [self.obT.b])

    def passB(self, kind, own0, T):
        S, C, W, X, O = self.S, self.C, self.W, self.X, self.O
        cst = self.cst
        segs = self.segs(kind, T)
        x = self.xT[0]

        def load(name, dst):
            S.dma(S.pool, "ld_" + name, dst.h[:, :, 0:T], X[name].h.rearrange("(k p) t -> p k t", p=128)[:, :, own0:own0 + T],
                  reads=[X[name].b], writes=[dst.b])
        load("x1", x)
        load("o0", self.o0b)
        if kind == "P":
            load("oP", self.oPb)
        load("zg", self.zgb)
        load("sga", self.sgab)
        load("mb", self.mbb)
        o = self.o0b
        if kind == "P":
            for h in range(8):
                t, c0, bufs = self.ps(1)
                self.mm(t[:, c0:c0 + T], self.S0b.h[:, h, :], self.oPb.h[:, h, 0:T], True, True, [self.S0b.b, self.oPb.b], bufs)
                self.tt(o.h[:, h, 0:T], o.h[:, h, 0:T], t[:, c0:c0 + T], ALU.add, [o.b] + bufs, [o.b])
        sq = self.hq
        self.act(sq.h[:, :, 0:T], o.h[:, :, 0:T], AF.Square, [o.b], [sq.b])
        for h in range(8):
            t, c0, bufs = self.ps(1)
            self.mm(t[:, c0:c0 + T], self.onesb.h[:, :], sq.h[:, h, 0:T], True, True, [self.onesb.b, sq.b], bufs)
            rr = self.tmpc[h % 2]
            self.act(rr.h[:, 0:T], t[:, c0:c0 + T], AF.Sqrt, bufs + [cst.b], [rr.b], bias=C("eps"), scale=1.0 / 128)
            self.recip(rr.h[:, 0:T], rr.h[:, 0:T], [rr.b], [rr.b])
            self.tt(rr.h[:, 0:T], rr.h[:, 0:T], o.h[:, h, 0:T], ALU.mult, [rr.b, o.b], [rr.b])
            self.stt(self.onT.h[:, h, 0:T], rr.h[:, 0:T], C("gnw"), self.zgb.h[:, h, 0:T], ALU.mult, ALU.mult, [rr.b, cst.b, self.zgb.b], [self.onT.b])
        def f_wa(c, t, c0, bufs):
            tm = self.tmpc[c % 2]
            self.tt(tm.h[:, 0:T], t[:, c0:c0 + T], self.sgab.h[:, c, 0:T], ALU.mult, bufs + [self.sgab.b], [tm.b])
            self.tt(self.mgT.h[:, c, 0:T], tm.h[:, 0:T], self.mbb.h[:, c, 0:T], ALU.add, [tm.b, self.mbb.b], [self.mgT.b])
        self.proj("wa", self.onT, T, f_wa)
        A2, sh2, G2 = self.modv[2]

        def f_wo(c, t, c0, bufs):
            for (s0, sl, mi) in segs:
                self.stt(x.h[:, c, s0:s0 + sl], t[:, c0 + s0:c0 + s0 + sl], G2.h[:, c, mi:mi + 1], x.h[:, c, s0:s0 + sl], ALU.mult, ALU.add,
                         bufs + [G2.b, x.b], [x.b])
        self.proj("wo", self.mgT, T, f_wo)
        self.ffn(x, T, segs, 3, "w2i", "w2o")
        y = self.o0b
        sqv = self.hq
        self.act(sqv.h[:, :, 0:T], x.h[:, :, 0:T], AF.Square, [x.b], [sqv.b])
        t, c0, bufs = self.ps(1)
        for k in range(KC):
            self.mm(t[:, c0:c0 + T], self.onesb.h[:, :], sqv.h[:, k, 0:T], k == 0, k == KC - 1, [self.onesb.b, sqv.b], bufs)
        rs = self.rs
        self.act(rs.h[:, 0:T], t[:, c0:c0 + T], AF.Sqrt, bufs + [cst.b], [rs.b], bias=C("eps"), scale=1.0 / D)
        self.recip(rs.h[:, 0:T], rs.h[:, 0:T], [rs.b], [rs.b])
        for c in range(KC):
            self.stt(y.h[:, c, 0:T], x.h[:, c, 0:T], C("normf", c, c + 1), rs.h[:, 0:T], ALU.mult, ALU.mult, [x.b, cst.b, rs.b], [y.b])
        S.dma(S.pool, "o_y", O["yT"].h.rearrange("(k p) t -> p k t", p=128)[:, :, own0:own0 + T], y.h[:, :, 0:T], reads=[y.b], writes=[O["yT"].b])


def _t5_bucket(rel):
    half = 16
    max_exact = 8
    n = np.abs(rel)
    lg = np.log(np.maximum(n, 1).astype(np.float32) / np.float32(max_exact)).astype(np.float32)
    large = max_exact + (lg / np.float32(np.log(128 / max_exact)) * np.float32(half - max_exact)).astype(np.int32)
    large = np.minimum(large, half - 1)
    return np.where(rel > 0, half, 0) + np.where(n < max_exact, n, large)


def _consts():
    c = np.zeros((128, NCST), np.float32)

    def put(name, arr):
        o, w = CST[name]
        c[:, o:o + w] = arr
    idx = np.arange(128)
    same = (idx[:, None] // 64) == (idx[None, :] // 64)
    put("ident", np.eye(128))
    put("ublk", ((idx[:, None] <= idx[None, :]) & same).astype(np.float32))
    put("blk", same.astype(np.float32))
    put("maskl", np.where((idx[None, :] < idx[:, None]) & same, 0.0, NEG))
    put("masku", np.where((idx[None, :] >= idx[:, None]) & same, 0.0, NEG))
    ob = np.zeros((128, 2, 128), np.float32)
    ob[0:64, 0, :] = 1.0
    ob[64:128, 1, :] = 1.0
    put("onesblk", ob.reshape(128, 256))
    put("eps", EPS)
    put("one", 1.0)
    put("eps128", EPS * 128.0)
    return c


_NC_CACHE = {}


def _get_nc():
    if "nc" not in _NC_CACHE:
        b = Builder()
        _NC_CACHE["nc"] = b.build()
        _NC_CACHE["stats"] = (b.S.n_ins, [(e.name, e.total, e.nwaits) for e in b.S.engs])
    return _NC_CACHE["nc"]


def _prep(x_prompt, x_sample, state_gdn_conv, state_gdn_s, cache_swa_k, cache_swa_v, c_prompt, c_sample,
           norm_ffn1, w_ffn1_in, w_ffn1_out, norm_mix, w_in, gdn_conv_w, gdn_a_log, gdn_dt_bias, gdn_norm_w,
           swa_sinks, rel_bias, w_branch_a, w_branch_b, w_out, norm_ffn2, w_ffn2_in, w_ffn2_out,
           w_ada, b_ada, norm_final):
    f = lambda a: np.ascontiguousarray(np.asarray(a, dtype=np.float32))
    xp = f(x_prompt)[0]
    xs = f(x_sample)
    base = _consts()

    def chunked(v):
        v = f(v).reshape(-1, 128)
        return v.T
    put = {}
    o, w = CST["norm1"]; base[:, o:o + w] = chunked(f(norm_ffn1)[0])
    o, w = CST["norm2"]; base[:, o:o + w] = chunked(f(norm_mix)[0])
    o, w = CST["norm3"]; base[:, o:o + w] = chunked(f(norm_ffn2)[0])
    o, w = CST["normf"]; base[:, o:o + w] = chunked(f(norm_final))
    o, w = CST["bada"]; base[:, o:o + w] = chunked(f(b_ada)[0])
    cwv = f(gdn_conv_w)[0]
    o, w = CST["convw"]; base[:, o:o + w] = cwv.T.reshape(24, 128, 4).transpose(1, 0, 2).reshape(128, 96)
    o, w = CST["dtb"]; base[:, o:o + w] = f(gdn_dt_bias)[0][None, :]
    o, w = CST["alog"]; base[:, o:o + w] = f(gdn_a_log)[0][None, :]
    o, w = CST["gnw"]; base[:, o:o + w] = f(gdn_norm_w)[0][:, None]
    o, w = CST["sinks"]; base[:, o:o + w] = f(swa_sinks)[0][None, :]
    sk = f(swa_sinks)[0]
    esk = np.zeros((16,), np.float32)
    for kv in range(2):
        for par in range(2):
            for j in range(4):
                esk[kv * 8 + par * 4 + j] = sk[kv * 8 + 2 * j + par]
    esk = np.broadcast_to(esk[None, :], (128, 16)).copy()
    rel = np.arange(192)[None, :] - 128 - np.arange(64)[:, None]
    bk = _t5_bucket(rel)
    oh = (np.arange(32)[:, None, None] == bk[None, :, :]).astype(np.float32).reshape(32, 64 * 192)
    relb = f(rel_bias)

    w1i, w1o, win = f(w_ffn1_in)[0], f(w_ffn1_out)[0], f(w_in)[0]
    wa, wb, wo = f(w_branch_a)[0], f(w_branch_b)[0], f(w_out)[0]
    w2i, w2o, wada = f(w_ffn2_in)[0], f(w_ffn2_out)[0], f(w_ada)[0]
    sconv = f(state_gdn_conv)[0]
    sS = f(state_gdn_s)[0]
    ck = f(cache_swa_k)[0]
    cv = f(cache_swa_v)[0]
    cpv = f(c_prompt)
    csv = f(c_sample)

    in_maps = []
    for r in range(NCORES):
        cst = base.copy()
        o, w = CST["hmask"]; cst[:, o:o + w] = NEG if r == 0 else 0.0
        o, w = CST["flag"]; cst[:, o:o + w] = 0.0 if r == 0 else 1.0
        o, w = CST["rankmask"]; cst[:, o:o + w] = (np.arange(8) < r).astype(np.float32)[None, :]
        c5 = np.concatenate([cpv, csv[4 * r:4 * r + 4]], 0)
        o, w = CST["cT"]; cst[:, o:o + w] = c5.T.reshape(8, 128, 5).transpose(1, 0, 2).reshape(128, 40)
        xT = np.zeros((D, HALO + NPR + NSA), np.float32)
        if r > 0:
            xT[:, 0:HALO] = xp[r * NPR - HALO:r * NPR].T
        xT[:, HALO:HALO + NPR] = xp[r * NPR:(r + 1) * NPR].T
        xT[:, HALO + NPR:] = xs[4 * r:4 * r + 4].reshape(NSA, D).T
        sc = sconv[4 * r:4 * r + 4]
        convst = sc.transpose(2, 0, 1).reshape(24, 128, NSQ, 3).transpose(1, 0, 2, 3).reshape(128, 24 * NSQ * 3)
        s0 = sS[4 * r:4 * r + 4].transpose(2, 0, 1, 3).reshape(128, NSQ * 8 * 128)
        kk = ck[4 * r:4 * r + 4]
        kcA = kk.transpose(2, 3, 0, 1).reshape(128, NSQ * 128)
        kcB = kk[:, :, ::-1, :].transpose(2, 3, 0, 1).reshape(128, NSQ * 128)
        vv = cv[4 * r:4 * r + 4]
        vc = vv.transpose(1, 0, 2, 3).reshape(128, NSQ * 128)
        in_maps.append({
            "xT": xT, "w1i": w1i, "w1o": w1o, "win": win, "wa": wa, "wb": wb, "wo": wo, "w2i": w2i, "w2o": w2o,
            "wada": wada, "cst": cst, "oh": oh, "relb": relb, "esk": esk, "convst": np.ascontiguousarray(convst),
            "s0": np.ascontiguousarray(s0), "kcA": np.ascontiguousarray(kcA), "kcB": np.ascontiguousarray(kcB),
            "vc": np.ascontiguousarray(vc), "vc2": np.ascontiguousarray(np.roll(vc, -64, axis=0)), "kraw": np.ascontiguousarray(kk.reshape(NSQ * 128, 128)),
            "vraw": np.ascontiguousarray(vv.reshape(NSQ * 128, 128)),
        })
    return in_maps


def _post(R):
    y_prompt = np.concatenate([R[r]["yT"][:, 0:NPR].T for r in range(NCORES)], 0)[None]
    y_sample = np.concatenate([R[r]["yT"][:, NPR:].T.reshape(NSQ, LSQ, D) for r in range(NCORES)], 0)
    L = NCORES - 1
    p_conv = R[L]["conv_p"].reshape(128, 24, 3).transpose(2, 1, 0).reshape(3, 3072)[None, None]
    p_s = R[L]["s_p"].reshape(128, 8, 128).transpose(1, 0, 2)[None, None]
    p_k = R[L]["k_p"].T.reshape(128, 2, 64)[None, None]
    p_v = R[L]["v_p"].reshape(128, 2, 64)[None, None]
    s_conv = np.concatenate([R[r]["conv_s"].reshape(128, 24, NSQ, 3).transpose(2, 3, 1, 0).reshape(NSQ, 3, 3072) for r in range(NCORES)], 0)[None]
    s_s = np.concatenate([R[r]["s_s"].reshape(128, NSQ, 8, 128).transpose(1, 2, 0, 3) for r in range(NCORES)], 0)[None]
    ks, vs = [], []
    for r in range(NCORES):
        kh = R[r]["kh_s"].reshape(NSQ, 64, 2, 64)
        kn = R[r]["kn_s"].T.reshape(NSQ, 64, 2, 64)
        ks.append(np.concatenate([kh, kn], 1))
        vh = R[r]["vh_s"].reshape(NSQ, 64, 2, 64)
        vn = R[r]["vn_s"].reshape(128, 2, 128).transpose(1, 0, 2).reshape(NSQ, 64, 2, 64)
        vs.append(np.concatenate([vh, vn], 1))
    s_k = np.concatenate(ks, 0)[None]
    s_v = np.concatenate(vs, 0)[None]
    outs = (y_prompt, y_sample, p_conv, p_s, p_k, p_v, s_conv, s_s, s_k, s_v)
    return tuple(np.ascontiguousarray(o_, dtype=np.float32) for o_ in outs)


def kernel(**inputs):
    in_maps = _prep(**inputs)
    nc = _get_nc()
    res = run_bass_kernel_spmd(nc, in_maps, core_ids=list(range(NCORES)))
    return _post(res.results)
```
